# Optimizing a Trainium2 kernel written in Bass

```python
import math
import jax
import jax.numpy as jnp
from jax import lax
import numpy as np

D_MODEL = 2048
BATCH = 8
SEQ = 2048
DEPTH = 2
DEC_BATCH = 32
DEC_SEQ = 8
PAST_LEN = 8192
PAGE_SIZE = 128

EPS = 1e-6
NEG = -1e30
N_EVEN = (DEPTH + 1) // 2
N_ODD = DEPTH // 2
D_CONV = D_MODEL // 2
CONV_W = 31
HEAD_DIM = 128
DIL_GROUPS = ((128, 1), (512, 4), (2048, 16))
N_DGROUPS = len(DIL_GROUPS)
HEADS_PER_GROUP = 4
D_ATT = N_DGROUPS * HEADS_PER_GROUP * HEAD_DIM
D_ATT_OUT = HEADS_PER_GROUP * HEAD_DIM
CHUNK = 128
D_GATE = D_MODEL
N_SG = 8
D_SG = D_GATE // N_SG
N_MEM = 256
MEM_HEADS = 4
MEM_HEAD_DIM = 128
D_MEMATT = MEM_HEADS * MEM_HEAD_DIM
D_FF = 4 * D_MODEL

kernel_name = 'hybrid_conv_dilated_gmlp_decoder_step'


def rmsnorm(x, g):
    xf = x.astype(jnp.float32)
    y = xf * lax.rsqrt(jnp.mean(xf * xf, axis=-1, keepdims=True) + EPS)
    return (y * g.astype(jnp.float32)).astype(x.dtype)


def layernorm(x, g, b):
    xf = x.astype(jnp.float32)
    xc = xf - jnp.mean(xf, axis=-1, keepdims=True)
    y = xc * lax.rsqrt(jnp.mean(xc * xc, axis=-1, keepdims=True) + EPS)
    return (y * g.astype(jnp.float32) + b.astype(jnp.float32)).astype(x.dtype)


def causal_dwconv(xpad, w, b):
    y = lax.conv_general_dilated(xpad, w[:, None, :].astype(xpad.dtype), window_strides=(1,), padding='VALID',
                                 dimension_numbers=('NWC', 'WIO', 'NWC'), feature_group_count=xpad.shape[-1])
    return y + b


def dilated_attn_prompt(q, k, v, window, dil):
    B, T, H, Dh = q.shape
    n = window // dil
    S = T // dil
    nb = -(-S // n)
    Sp = nb * n

    def to_sub(a):
        a = a.reshape(B, S, dil, H, Dh).transpose(0, 2, 1, 3, 4)
        return jnp.pad(a, ((0, 0), (0, 0), (0, Sp - S), (0, 0), (0, 0)))

    def band(a):
        a = jnp.pad(to_sub(a), ((0, 0), (0, 0), (n, 0), (0, 0), (0, 0))).reshape(B, dil, nb + 1, n, H, Dh)
        return jnp.concatenate([a[:, :, :-1], a[:, :, 1:]], axis=3)

    qs = to_sub(q).reshape(B, dil, nb, n, H, Dh)
    kb, vb = band(k), band(v)
    s = jnp.einsum('brcqhd,brckhd->brchqk', qs, kb, preferred_element_type=jnp.float32)
    c = jnp.arange(nb)[:, None, None]
    qi = jnp.arange(n)[None, :, None]
    kj = jnp.arange(2 * n)[None, None, :]
    dist = qi + n - kj
    mask = (dist >= 0) & (dist < n) & (c * n - n + kj >= 0)
    s = jnp.where(mask[:, None], s, NEG)
    lse = jax.nn.logsumexp(s, axis=-1)
    p = jnp.exp(s - lse[..., None])
    o = jnp.einsum('brchqk,brckhd->brcqhd', p.astype(v.dtype), vb)
    o = o.reshape(B, dil, Sp, H, Dh)[:, :, :S].transpose(0, 2, 1, 3, 4).reshape(B, T, H, Dh)
    lse = lse.transpose(0, 1, 2, 4, 3).reshape(B, dil, Sp, H)[:, :, :S].transpose(0, 2, 1, 3).reshape(B, T, H)
    return o, lse


def dilated_attn_sample(q, k_cache, v_cache, k_new, v_new, window, dil):
    B, DS, H, Dh = q.shape
    L = k_cache.shape[1]
    n = window // dil
    k_all = jnp.concatenate([k_cache, k_new], axis=1)
    v_all = jnp.concatenate([v_cache, v_new], axis=1)
    idx = L + jnp.arange(DS)[:, None] - dil * jnp.arange(n)[None, :]
    valid = idx >= 0
    idx = jnp.maximum(idx, 0)
    kg, vg = k_all[:, idx], v_all[:, idx]
    s = jnp.einsum('bqhd,bqnhd->bqhn', q, kg, preferred_element_type=jnp.float32)
    s = jnp.where(valid[None, :, None, :], s, NEG)
    lse = jax.nn.logsumexp(s, axis=-1)
    p = jnp.exp(s - lse[..., None])
    o = jnp.einsum('bqhn,bqnhd->bqhd', p.astype(vg.dtype), vg)
    return o, lse


def even_mixer(h, w_in, conv_w, conv_b, ln_g, ln_b, qn_g, kn_g, w_out, conv_state=None, kv_cache=None):
    B, T, _ = h.shape
    z = h @ w_in
    a = z[..., :D_CONV] * jax.nn.sigmoid(z[..., D_CONV:2 * D_CONV])
    if conv_state is None:
        apad = jnp.pad(a, ((0, 0), (CONV_W - 1, 0), (0, 0)))
    else:
        apad = jnp.concatenate([conv_state, a], axis=1)
    new_conv = apad[:, -(CONV_W - 1):]
    a_out = jax.nn.silu(layernorm(causal_dwconv(apad, conv_w, conv_b), ln_g, ln_b))
    qkv = z[..., 2 * D_CONV:].reshape(B, T, 3, N_DGROUPS, HEADS_PER_GROUP, HEAD_DIM)
    q = rmsnorm(qkv[:, :, 0], qn_g[:, None, :]) * (HEAD_DIM ** -0.5)
    k = rmsnorm(qkv[:, :, 1], kn_g[:, None, :])
    v = qkv[:, :, 2]
    outs, lses, new_kv = [], [], []
    for gi, (win, dil) in enumerate(DIL_GROUPS):
        qg, kg, vg = q[:, :, gi], k[:, :, gi], v[:, :, gi]
        if kv_cache is None:
            o, l = dilated_attn_prompt(qg, kg, vg, win, dil)
            keep = min(win, T)
            new_kv += [kg[:, T - keep:], vg[:, T - keep:]]
        else:
            o, l = dilated_attn_sample(qg, kv_cache[2 * gi], kv_cache[2 * gi + 1], kg, vg, win, dil)
            new_kv += [kg, vg]
        outs.append(o)
        lses.append(l)
    wts = jax.nn.softmax(jnp.stack(lses), axis=0)
    b_out = jnp.einsum('gbth,gbthd->bthd', wts, jnp.stack(outs).astype(jnp.float32)).astype(h.dtype)
    y = jnp.concatenate([a_out, b_out.reshape(B, T, D_ATT_OUT)], axis=-1) @ w_out
    return y, new_conv, new_kv


def odd_mixer(h, w_in, b_in, vln_g, vln_b, w_s, b_s, w_out):
    B, T, _ = h.shape
    z = jax.nn.gelu(h @ w_in + b_in, approximate=False)
    u = z[..., :D_GATE]
    v = layernorm(z[..., D_GATE:], vln_g, vln_b)
    C = min(CHUNK, T)
    ws = w_s[:, :C, :C] * jnp.tril(jnp.ones((C, C), w_s.dtype))
    vc = v.reshape(B, T // C, C, N_SG, D_SG)
    sv = jnp.einsum('gts,bcsgk->bctgk', ws, vc) + b_s[:, :C].T[None, None, :, :, None]
    y = (u * sv.reshape(B, T, D_GATE)) @ w_out
    return y, v


def mem_kv(mem, g_mem, wk, wv, kn):
    B, N, _ = mem.shape
    m = rmsnorm(mem, g_mem)
    k = rmsnorm((m @ wk).reshape(B, N, MEM_HEADS, MEM_HEAD_DIM), kn)
    v = (m @ wv).reshape(B, N, MEM_HEADS, MEM_HEAD_DIM)
    return k, v


def mem_attn(h, k, v, wq, qn, wo):
    B, T, _ = h.shape
    q = rmsnorm((h @ wq).reshape(B, T, MEM_HEADS, MEM_HEAD_DIM), qn) * (MEM_HEAD_DIM ** -0.5)
    s = jnp.einsum('bthd,bnhd->bhtn', q, k, preferred_element_type=jnp.float32)
    p = jax.nn.softmax(s, axis=-1)
    o = jnp.einsum('bhtn,bnhd->bthd', p.astype(v.dtype), v).reshape(B, T, D_MEMATT)
    return o @ wo


def sq_relu_mlp(h, w1, w2):
    return jnp.square(jax.nn.relu(h @ w1)) @ w2


def setup_inputs(seed: int = 0) -> dict:
    key = jax.random.key(seed)
    ks = list(jax.random.split(key, 48))

    def nrm(shape, scale):
        return scale * jax.random.normal(ks.pop(), shape, jnp.float32)

    def gain(shape):
        return 1.0 + nrm(shape, 0.02)

    d = {}
    d['x_prompt'] = nrm((BATCH, SEQ, D_MODEL), 1.0)
    d['x_sample'] = nrm((DEC_BATCH, DEC_SEQ, D_MODEL), 1.0)
    d['mem_prompt'] = nrm((BATCH, N_MEM, D_MODEL), 1.0)
    d['state_conv'] = nrm((N_EVEN, DEC_BATCH, CONV_W - 1, D_CONV), 0.5)
    for win, _ in DIL_GROUPS:
        L = min(win, PAST_LEN)
        d['cache_k_w%d' % win] = nrm((N_EVEN, DEC_BATCH, L, HEADS_PER_GROUP, HEAD_DIM), 1.0)
        d['cache_v_w%d' % win] = nrm((N_EVEN, DEC_BATCH, L, HEADS_PER_GROUP, HEAD_DIM), 1.0)
    d['cache_mem_k'] = nrm((DEPTH, DEC_BATCH, N_MEM, MEM_HEADS, MEM_HEAD_DIM), 1.0)
    d['cache_mem_v'] = nrm((DEPTH, DEC_BATCH, N_MEM, MEM_HEADS, MEM_HEAD_DIM), 1.0)
    d['g_mix'] = gain((DEPTH, D_MODEL))
    d['w_in_e'] = nrm((N_EVEN, D_MODEL, 2 * D_CONV + 3 * D_ATT), D_MODEL ** -0.5)
    d['conv_w'] = nrm((N_EVEN, CONV_W, D_CONV), CONV_W ** -0.5)
    d['conv_b'] = nrm((N_EVEN, D_CONV), 0.02)
    d['conv_ln_g'] = gain((N_EVEN, D_CONV))
    d['conv_ln_b'] = nrm((N_EVEN, D_CONV), 0.02)
    d['q_norm_e'] = gain((N_EVEN, N_DGROUPS, HEAD_DIM))
    d['k_norm_e'] = gain((N_EVEN, N_DGROUPS, HEAD_DIM))
    d['w_out_e'] = nrm((N_EVEN, D_CONV + D_ATT_OUT, D_MODEL), (D_CONV + D_ATT_OUT) ** -0.5)
    d['w_in_o'] = nrm((N_ODD, D_MODEL, 2 * D_GATE), D_MODEL ** -0.5)
    d['b_in_o'] = nrm((N_ODD, 2 * D_GATE), 0.02)
    d['v_ln_g'] = gain((N_ODD, D_GATE))
    d['v_ln_b'] = nrm((N_ODD, D_GATE), 0.02)
    d['w_s'] = nrm((N_ODD, N_SG, CHUNK, CHUNK), CHUNK ** -0.5)
    d['b_s'] = gain((N_ODD, N_SG, CHUNK))
    d['w_out_o'] = nrm((N_ODD, D_GATE, D_MODEL), D_GATE ** -0.5)
    d['g_xmem'] = gain((DEPTH, D_MODEL))
    d['g_mem'] = gain((DEPTH, D_MODEL))
    d['wq_mem'] = nrm((DEPTH, D_MODEL, D_MEMATT), D_MODEL ** -0.5)
    d['wk_mem'] = nrm((DEPTH, D_MODEL, D_MEMATT), D_MODEL ** -0.5)
    d['wv_mem'] = nrm((DEPTH, D_MODEL, D_MEMATT), D_MODEL ** -0.5)
    d['q_norm_mem'] = gain((DEPTH, MEM_HEAD_DIM))
    d['k_norm_mem'] = gain((DEPTH, MEM_HEAD_DIM))
    d['wo_mem'] = nrm((DEPTH, D_MEMATT, D_MODEL), D_MEMATT ** -0.5)
    d['g_ffn'] = gain((DEPTH, D_MODEL))
    d['w_ffn1'] = nrm((DEPTH, D_MODEL, D_FF), D_MODEL ** -0.5)
    d['w_ffn2'] = nrm((DEPTH, D_FF, D_MODEL), D_FF ** -0.5)
    return d


def reference(x_prompt, x_sample, mem_prompt, state_conv, cache_k_w128, cache_v_w128, cache_k_w512, cache_v_w512,
              cache_k_w2048, cache_v_w2048, cache_mem_k, cache_mem_v, g_mix, w_in_e, conv_w, conv_b, conv_ln_g,
              conv_ln_b, q_norm_e, k_norm_e, w_out_e, w_in_o, b_in_o, v_ln_g, v_ln_b, w_s, b_s, w_out_o, g_xmem,
              g_mem, wq_mem, wk_mem, wv_mem, q_norm_mem, k_norm_mem, wo_mem, g_ffn, w_ffn1, w_ffn2):
    kv_in = (cache_k_w128, cache_v_w128, cache_k_w512, cache_v_w512, cache_k_w2048, cache_v_w2048)
    xp, xs = x_prompt, x_sample
    conv_pl, conv_sl, kv_pl, kv_sl, memk_pl, memv_pl, chunk_sl = [], [], [], [], [], [], []
    for i in range(DEPTH):
        j = i // 2
        hp = rmsnorm(xp, g_mix[i])
        hs = rmsnorm(xs, g_mix[i])
        if i % 2 == 0:
            ew = (w_in_e[j], conv_w[j], conv_b[j], conv_ln_g[j], conv_ln_b[j], q_norm_e[j], k_norm_e[j], w_out_e[j])
            yp, cp, kvp = even_mixer(hp, *ew)
            ys, cs, kvs = even_mixer(hs, *ew, conv_state=state_conv[j], kv_cache=tuple(c[j] for c in kv_in))
            conv_pl.append(cp)
            conv_sl.append(cs)
            kv_pl.append(kvp)
            kv_sl.append(kvs)
        else:
            ow = (w_in_o[j], b_in_o[j], v_ln_g[j], v_ln_b[j], w_s[j], b_s[j], w_out_o[j])
            yp, _ = odd_mixer(hp, *ow)
            ys, vs = odd_mixer(hs, *ow)
            chunk_sl.append(vs)
        xp = xp + yp
        xs = xs + ys
        mk, mv = mem_kv(mem_prompt, g_mem[i], wk_mem[i], wv_mem[i], k_norm_mem[i])
        memk_pl.append(mk)
        memv_pl.append(mv)
        xp = xp + mem_attn(rmsnorm(xp, g_xmem[i]), mk, mv, wq_mem[i], q_norm_mem[i], wo_mem[i])
        xs = xs + mem_attn(rmsnorm(xs, g_xmem[i]), cache_mem_k[i], cache_mem_v[i], wq_mem[i], q_norm_mem[i], wo_mem[i])
        xp = xp + sq_relu_mlp(rmsnorm(xp, g_ffn[i]), w_ffn1[i], w_ffn2[i])
        xs = xs + sq_relu_mlp(rmsnorm(xs, g_ffn[i]), w_ffn1[i], w_ffn2[i])
    conv_p = jnp.stack(conv_pl)
    conv_s = jnp.stack(conv_sl)
    k128_p, v128_p, k512_p, v512_p, k2048_p, v2048_p = [jnp.stack([kv[n] for kv in kv_pl]) for n in range(2 * N_DGROUPS)]
    k128_s, v128_s, k512_s, v512_s, k2048_s, v2048_s = [jnp.stack([kv[n] for kv in kv_sl]) for n in range(2 * N_DGROUPS)]
    memk_p = jnp.stack(memk_pl)
    memv_p = jnp.stack(memv_pl)
    chunkv_s = jnp.stack(chunk_sl)
    return (xp, xs, conv_p, conv_s, k128_p, v128_p, k512_p, v512_p, k2048_p, v2048_p,
            k128_s, v128_s, k512_s, v512_s, k2048_s, v2048_s, memk_p, memv_p, chunkv_s)
```

```python
import numpy as np
import concourse.bass as bass
import concourse.mybir as mybir
from concourse.bass_utils import run_bass_kernel_spmd

F32 = mybir.dt.float32
BF16 = mybir.dt.bfloat16
AF = mybir.ActivationFunctionType
ALU = mybir.AluOpType
AX = mybir.AxisListType

_DSZ = {F32: 4, BF16: 2, mybir.dt.int32: 4}


class _Op:
    __slots__ = ("eng", "fn", "deps", "dma", "semkey", "count", "needs_inc", "idx", "src")


def _skip_same_engine(kind, eng):
    if kind == "drain":
        return False
    return eng == "pe"


def _pe_mode(lhsT):
    def r(v):
        return 32 if v <= 32 else (64 if v <= 64 else 128)
    k = lhsT.shape[0]
    m = 1
    for d_ in lhsT.shape[1:]:
        m *= d_
    return (r(k), r(m))


class Sched:
    ENGS = ("pe", "act", "dve", "pool", "sp")

    def __init__(self, nc):
        self.nc = nc
        self.ops = []
        self.recs = {}
        self.track_dram = set()
        self.dma_sem_count = {}
        self.dma_sem_queue = {}
        self.last_pe = None
        self.psum_last = {}

    def box(self, ap):
        t = ap.tensor
        name = t.name
        apl = ap.ap
        off = int(ap.offset)
        dsz = _DSZ[ap.dtype]
        isdram = name in self.track_dram
        if isdram:
            ext = 0
            for st, cn in apl:
                ext += (cn - 1) * abs(st)
            return name, 0, 1, off * dsz, (off + ext + 1) * dsz
        row = apl[0][0]
        if row == 0:
            row = 1 << 30
        p0 = off // row
        f0 = off % row
        ext = 0
        for st, cn in apl[1:]:
            ext += (cn - 1) * abs(st)
        return name, p0, p0 + apl[0][1], f0 * dsz, (f0 + ext + 1) * dsz

    def add(self, eng, fn, reads=(), writes=(), dma=False, semkey=None, pe_mode=None):
        op = _Op()
        op.eng = eng
        op.fn = fn
        op.dma = dma
        op.semkey = semkey
        op.count = None
        op.needs_inc = False
        op.idx = len(self.ops)
        try:
            import sys as _sys
            f = _sys._getframe(2)
            op.src = (f.f_lineno, f.f_back.f_lineno if f.f_back else 0)
        except Exception:
            op.src = None
        deps = {}
        psum_acc = {}
        for ap in reads:
            if ap is not None and str(ap.space) == "PSUM":
                psum_acc.setdefault(ap.tensor.name, False)
        for ap in writes:
            if ap is not None and str(ap.space) == "PSUM":
                psum_acc[ap.tensor.name] = True
        for bname, is_w in psum_acc.items():
            st_ = self.psum_last.setdefault(bname, {})
            for e_, (a_idx, a_w) in st_.items():
                if e_ == eng:
                    if a_w and not is_w:
                        deps[a_idx] = "raw"
                    continue
                deps[a_idx] = "raw" if (a_w and not is_w) else "war"
            st_[eng] = (op.idx, is_w)
        reads = [ap for ap in reads if ap is not None and str(ap.space) != "PSUM"]
        writes = [ap for ap in writes if ap is not None and str(ap.space) != "PSUM"]
        for ap in reads:
            if ap is None:
                continue
            if ap.tensor.name not in self.recs and not self._tracked(ap):
                continue
            name, p0, p1, f0, f1 = self.box(ap)
            lst = self.recs.setdefault(name, [])
            for r in lst:
                if r[5] and r[0] < p1 and p0 < r[1] and r[2] < f1 and f0 < r[3]:
                    deps[r[4]] = "raw"
            merged = False
            if not dma:
                for r in lst:
                    if (not r[5]) and r[0] == p0 and r[1] == p1 and r[2] == f0 and r[3] == f1:
                        if r[4] == op.idx:
                            merged = True
                            break
                        if not (self.ops[r[4]].eng == eng and not self.ops[r[4]].dma):
                            continue
                        r[4] = op.idx
                        merged = True
                        break
            if not merged:
                lst.append([p0, p1, f0, f1, op.idx, False])
        for ap in writes:
            if ap is None:
                continue
            if not self._tracked(ap):
                continue
            name, p0, p1, f0, f1 = self.box(ap)
            lst = self.recs.setdefault(name, [])
            keep = []
            for r in lst:
                if r[0] < p1 and p0 < r[1] and r[2] < f1 and f0 < r[3]:
                    if r[4] != op.idx:
                        if r[4] not in deps:
                            deps[r[4]] = "waw" if r[5] else "war"
                    if r[0] >= p0 and r[1] <= p1 and r[2] >= f0 and r[3] <= f1:
                        continue
                keep.append(r)
            keep.append([p0, p1, f0, f1, op.idx, True])
            self.recs[name] = keep
        if eng == "pe":
            if self.last_pe is not None and pe_mode != self.last_pe[1]:
                deps[self.last_pe[0]] = "drain"
            self.last_pe = (op.idx, pe_mode)
        if len(deps) > 1:
            best = {}
            red = {}
            for a_idx, kind in deps.items():
                a = self.ops[a_idx]
                if a.dma:
                    red[a_idx] = kind
                    continue
                if (not dma) and a.eng == eng:
                    if _skip_same_engine(kind, eng):
                        continue
                cur = best.get(a.eng)
                if cur is None or a_idx > cur:
                    best[a.eng] = a_idx
            for e_, a_idx in best.items():
                kd = deps[a_idx]
                if e_ == eng and eng == "pe":
                    kd = "drain"
                red[a_idx] = kd
            bysem = {}
            for a_idx in list(red):
                a = self.ops[a_idx]
                if a.dma:
                    cur = bysem.get(a.semkey)
                    if cur is None or a_idx > cur:
                        if cur is not None:
                            del red[cur]
                        bysem[a.semkey] = a_idx
                    else:
                        del red[a_idx]
            deps = red
        op.deps = deps
        if dma:
            n = self.dma_sem_count.get(semkey, 0) + 1
            self.dma_sem_count[semkey] = n
            op.count = 16 * n
            q = self.dma_sem_queue.setdefault(semkey, eng)
            assert q == eng, "one DMA semaphore must be fed by a single queue"
        self.ops.append(op)
        return op

    def _tracked(self, ap):
        sp = str(ap.space)
        if "DRAM" in sp.upper() or "HBM" in sp.upper():
            return ap.tensor.name in self.track_dram
        return True

    def emit(self):
        nc = self.nc
        ops = self.ops
        for b in ops:
            for a_idx, kind in b.deps.items():
                a = ops[a_idx]
                if a.dma:
                    continue
                if (not b.dma) and a.eng == b.eng:
                    if _skip_same_engine(kind, a.eng):
                        continue
                a.needs_inc = True
        cnt = {e: 0 for e in self.ENGS}
        for a in ops:
            if (not a.dma) and a.needs_inc:
                cnt[a.eng] += 1
                a.count = cnt[a.eng]
        esem = {e: nc.alloc_semaphore("s_" + e) for e in self.ENGS}
        dsem = {k: nc.alloc_semaphore("d_%s" % (str(k),)) for k in self.dma_sem_count}
        self.esem, self.dsem = esem, dsem
        by_eng = {e: [] for e in self.ENGS}
        for o in ops:
            by_eng[o.eng].append(o)
        nwaits = [0]

        def run(engobj, ename):
            waited = {}
            for b in by_eng[ename]:
                need = {}
                for a_idx, kind in b.deps.items():
                    a = ops[a_idx]
                    if a.dma:
                        s = dsem[a.semkey]
                        v = a.count
                    else:
                        if (not b.dma) and a.eng == b.eng and _skip_same_engine(kind, a.eng):
                            continue
                        s = esem[a.eng]
                        v = a.count
                    key = s.num
                    if need.get(key, (None, 0))[1] < v:
                        need[key] = (s, v)
                for key, (s, v) in need.items():
                    if waited.get(key, 0) >= v:
                        continue
                    waited[key] = v
                    engobj.wait_ge(s, v)
                    nwaits[0] += 1
                try:
                    ins = b.fn(engobj)
                except Exception:
                    print("FAILED op", b.idx, b.eng, "source lines", b.src)
                    raise
                if b.dma:
                    ins.then_inc(dsem[b.semkey], 16)
                elif b.needs_inc:
                    ins.then_inc(esem[b.eng], 1)
            if ename == "sp":
                for k, n in self.dma_sem_count.items():
                    engobj.wait_ge(dsem[k], 16 * n)

        with nc.Block() as block:
            @block.tensor
            def _(e):
                run(e, "pe")

            @block.scalar
            def _(e):
                run(e, "act")

            @block.vector
            def _(e):
                run(e, "dve")

            @block.gpsimd
            def _(e):
                run(e, "pool")

            @block.sync
            def _(e):
                run(e, "sp")
        self.nwaits = nwaits[0]


D = 2048
DC = 16
SEQ = 2048
TT = 512
NT = SEQ // TT
SB = 4
DS = 8
NS = SB * DS
DCONV = 1024
CW = 31
NMEM = 256
DFF = 8192
EPS = 1e-6
WCOLS = 128
NWSLOT = 3


class TG:
    def __init__(self, name, n, sample, tile=0):
        self.name, self.n, self.sample, self.tile = name, n, sample, tile


class Builder:
    def __init__(self, nc, stage=99):
        self.nc = nc
        self.s = Sched(nc)
        self.stage = stage
        self.wctr = 0
        self.wfctr = 0
        self.psctr = 0
        self.uid = 0
        self.dram = {}
        self.outs = {}

    def sb(self, name, shape, dt=F32):
        return self.nc.alloc_sbuf_tensor(name, list(shape), dt)

    def din(self, name, shape, dt=F32):
        t = self.nc.dram_tensor(name, list(shape), dt, kind="ExternalInput")
        self.dram[name] = t
        return t

    def dout(self, name, shape, dt=F32):
        t = self.nc.dram_tensor(name, list(shape), dt, kind="ExternalOutput")
        self.outs[name] = t
        return t

    def mm(self, out, lhsT, rhs, start=True, stop=True, skip=False):
        if skip:
            self.s.add("pe", lambda e: e.matmul(out, lhsT, rhs, start=start, stop=stop, skip_group_check=True),
                       reads=[lhsT, rhs], writes=[out], pe_mode=_pe_mode(lhsT))
        else:
            self.s.add("pe", lambda e: e.matmul(out, lhsT, rhs, start=start, stop=stop),
                       reads=[lhsT, rhs], writes=[out], pe_mode=_pe_mode(lhsT))

    def tr(self, out, in_, ident):
        self.s.add("pe", lambda e: e.transpose(out, in_, ident), reads=[in_, ident], writes=[out],
                   pe_mode=_pe_mode(in_))

    def act(self, out, in_, func, bias=None, scale=None, accum_out=None):
        kw = {}
        rd = [in_]
        if bias is not None:
            kw["bias"] = bias
            if not isinstance(bias, (int, float)):
                rd.append(bias)
        if scale is not None:
            kw["scale"] = scale
            if not isinstance(scale, (int, float)):
                rd.append(scale)
        wr = [out]
        if accum_out is not None:
            kw["accum_out"] = accum_out
            wr.append(accum_out)
        self.s.add("act", lambda e: e.activation(out, in_, func, **kw), reads=rd, writes=wr)

    def tt(self, eng, out, in0, in1, op):
        self.s.add(eng, lambda e: e.tensor_tensor(out, in0, in1, op), reads=[in0, in1], writes=[out])

    def ts(self, eng, out, in0, s1, s2, op0, op1=None):
        rd = [in0]
        if not isinstance(s1, (int, float)):
            rd.append(s1)
        if s2 is not None and not isinstance(s2, (int, float)):
            rd.append(s2)
        if op1 is None:
            self.s.add(eng, lambda e: e.tensor_scalar(out, in0, s1, None, op0), reads=rd, writes=[out])
        else:
            self.s.add(eng, lambda e: e.tensor_scalar(out, in0, s1, s2, op0, op1), reads=rd, writes=[out])

    def stt(self, eng, out, in0, scalar, in1, op0, op1):
        rd = [in0, in1]
        if not isinstance(scalar, (int, float)):
            rd.append(scalar)
        self.s.add(eng, lambda e: e.scalar_tensor_tensor(out, in0, scalar, in1, op0, op1), reads=rd, writes=[out])

    def cp(self, eng, out, in_):
        if eng == "act":
            self.s.add("act", lambda e: e.copy(out, in_), reads=[in_], writes=[out])
        else:
            self.s.add(eng, lambda e: e.tensor_copy(out, in_), reads=[in_], writes=[out])

    def memset(self, eng, out, val):
        self.s.add(eng, lambda e: e.memset(out, val), reads=[], writes=[out])

    def dma(self, q, out, in_, semkey):
        self.s.add(q, lambda e: e.dma_start(out=out, in_=in_), reads=[in_], writes=[out], dma=True, semkey=semkey)

    def recip(self, out, in_):
        self.s.add("dve", lambda e: e.reciprocal(out, in_), reads=[in_], writes=[out])

    def rsqrt_eps(self, out, in_, eps=EPS):
        self.act(out, in_, AF.Sqrt, bias=self.epsb[0:in_.shape[0], :], scale=1.0)
        self.recip(out, out)

    def psum(self):
        i = self.psctr % 4
        self.psctr += 1
        return self.ps[i]

    def wload(self, wap):
        K, ncols = wap.shape
        kc = K // 128
        i = self.wctr % NWSLOT
        self.wctr += 1
        dst = self.wslots[i][:, 0:kc, 0:ncols]
        src = wap.rearrange("(kc p) n -> p kc n", p=128)
        self.dma("pool", dst, src, ("w", i))
        return dst

    def setup(self):
        nc = self.nc
        self.ps = [nc.alloc_psum_tensor("ps%d" % i, [128, 512], F32) for i in range(4)]
        self.psA = nc.alloc_psum_tensor("psA", [128, 512], F32)
        self.psB = nc.alloc_psum_tensor("psB", [128, 512], F32)
        self.psC = nc.alloc_psum_tensor("psC", [128, 512], F32)
        self.psT = nc.alloc_psum_tensor("psT", [128, 1024], BF16)
        self.wslots = [self.sb("wslot%d" % i, [128, 16, WCOLS], BF16) for i in range(NWSLOT)]
        d_ident = self.din("c_ident", [128, 128])
        self.ident = self.sb("ident", [128, 128])
        self.dma("sp", self.ident[:], d_ident.ap(), "c0")
        self.identb = self.sb("identb", [128, 128], BF16)
        self.cp("dve", self.identb[:], self.ident[:])
        self.onesD = self.sb("onesD", [128, 128], BF16)
        self.memset("dve", self.onesD[:], 1.0 / 2048)
        self.onesC = self.sb("onesC", [128, 128], BF16)
        self.memset("dve", self.onesC[:], 1.0 / 1024)
        self.onesH = self.sb("onesH", [128, 128], BF16)
        self.memset("dve", self.onesH[:], 1.0 / 128)
        self.ones1 = self.sb("ones1", [128, 128], BF16)
        self.memset("dve", self.ones1[:], 1.0)
        self.epsb = self.sb("epsb", [128, 1])
        self.memset("dve", self.epsb[:], EPS)
        self.NV = 16 * 8 + 8 * 31 + 8 * 3 + 3 + 2 + 16 + 32
        d_vecs = self.din("vecs", [128, self.NV])
        self.vecs = self.sb("vecs_sb", [128, self.NV])
        self.dma("sp", self.vecs[:], d_vecs.ap(), "c1")
        o = 0
        v = self.vecs
        self.g_mix = [v[:, o + 16 * l:o + 16 * l + 16] for l in range(2)]; o += 32
        self.g_xmem = [v[:, o + 16 * l:o + 16 * l + 16] for l in range(2)]; o += 32
        self.g_ffn = [v[:, o + 16 * l:o + 16 * l + 16] for l in range(2)]; o += 32
        self.g_mem = [v[:, o + 16 * l:o + 16 * l + 16] for l in range(2)]; o += 32
        self.conv_w = v[:, o:o + 248]; o += 248
        self.conv_b = v[:, o:o + 8]; o += 8
        self.cln_g = v[:, o:o + 8]; o += 8
        self.cln_b = v[:, o:o + 8]; o += 8
        self.qn_e = v[:, o:o + 3]; o += 3
        self.qn_m = v[:, o:o + 2]; o += 2
        self.b_u = v[:, o:o + 16]; o += 16
        self.vln_g = v[:, o:o + 16]; o += 16
        self.vln_b = v[:, o:o + 16]; o += 16
        assert o == self.NV
        self.NR = 3 * 128 + 2 * 128 + 1024
        d_rows = self.din("rows", [128, self.NR])
        self.rows = self.sb("rows_sb", [128, self.NR])
        self.dma("sp", self.rows[:], d_rows.ap(), "c2")
        r = self.rows
        o = 0
        self.kn_e = [r[:, o + 128 * g:o + 128 * g + 128] for g in range(3)]; o += 384
        self.kn_m = [r[:, o + 128 * l:o + 128 * l + 128] for l in range(2)]; o += 256
        self.bs_row = r[:, o:o + 1024]; o += 1024
        assert o == self.NR
        self.NMK = 4 * 128 + 13 * 8 + 3 * 32 + 128
        d_masks = self.din("c_masks", [128, self.NMK], BF16)
        self.masks = self.sb("masks", [128, self.NMK], BF16)
        self.dma("sp", self.masks[:], d_masks.ap(), "c3")
        m = self.masks
        self.m_own = m[:, 0:128]
        self.m_prev = m[:, 128:256]
        self.m_bdf = m[:, 256:384]
        self.m_bdc = m[:, 384:512]
        self.m_sc = [m[:, 512 + 8 * i:520 + 8 * i] for i in range(13)]
        self.m_sn = [m[:, 616 + 32 * g:648 + 32 * g] for g in range(3)]
        self.m_low = m[:, 712:840]
        self.E0 = self.sb("E0", [128, 128], BF16)
        self.memset("dve", self.E0[:], 0.0)
        self.memset("dve", self.E0[0:1, :], 1.0)
        self.din("bvrow", [1, 2048])

    SCR_BYTES = 64 * 1024

    def carve(self, boff, shape, dt):
        n = 1
        for d_ in shape:
            n *= d_
        if dt == F32:
            assert boff % 4 == 0
            base = self.scr32[:, boff // 4:boff // 4 + n]
        else:
            assert boff % 2 == 0
            base = self.scr16[:, boff // 2:boff // 2 + n]
        assert boff + n * _DSZ[dt] <= self.SCR_BYTES, (boff, shape)
        if len(shape) == 1:
            return base
        if len(shape) == 2:
            return base.rearrange("p (a b) -> p a b", b=shape[1])
        return base.rearrange("p (a b c) -> p a b c", b=shape[1], c=shape[2])

    def alloc_main(self):
        scr = self.sb("scratch", [128, self.SCR_BYTES // 4], F32)
        self.scr32 = scr
        self.scr16 = scr.bitcast(BF16)
        self.x = self.sb("x_sb", [128, DC, TT])
        self.h = self.sb("h_sb", [128, DC, TT], BF16)
        self.xs = self.sb("xs_sb", [128, DC, NS])
        self.hs = self.sb("hs_sb", [128, DC, 128], BF16)
        self.memset("pool", self.hs[:], 0.0)
        self.tgP = [TG("p%d" % i, TT, False, i) for i in range(NT)]
        self.tgS = TG("s", NS, True)

    def X(self, tg):
        return self.xs if tg.sample else self.x

    def H(self, tg):
        return self.hs if tg.sample else self.h

    def load_x(self, tg, src):
        n = tg.n
        x = self.X(tg)
        nb = (n + 127) // 128
        for b in range(nb):
            rows = min(128, n - b * 128)
            st = self.carve((b % 2) * 8192, [D], F32)
            if rows < 128:
                self.memset("pool", st[:, :], 0.0)
            self.dma("sp", st[0:rows, :], src[b * 128:b * 128 + rows, :], ("xin", b % 2))
            for c4 in range(0, DC, 4):
                ps = self.psum()
                for j in range(4):
                    c = c4 + j
                    self.tr(ps[:, j * 128:(j + 1) * 128], st[:, c * 128:(c + 1) * 128], self.ident[:])
                eng = "act" if (c4 // 4) % 2 == 0 else "dve"
                src_v = ps[:, 0:512].rearrange("p (j t) -> p j t", t=128)[:, :, 0:rows]
                self.cp(eng, x[:, c4:c4 + 4, b * 128:b * 128 + rows], src_v)

    def fm_to_dram_small(self, src3, nchunk, w, dst_fn, semkey, st=None):
        ng = nchunk // 4
        if st is None:
            st = self.carve(0, [4, 128], F32)
        assert ng <= 4
        ps = self.psum()
        for g in range(ng):
            self.tr(ps[0:4 * w, g * 128:(g + 1) * 128], src3[:, 4 * g:4 * g + 4, :], self.ident[:])
        self.cp("dve", st[0:4 * w, 0:ng, :], ps[0:4 * w, 0:ng * 128].rearrange("p (g f) -> p g f", f=128))
        for cl in range(4):
            self.dma("sp", dst_fn(cl, ng), st[cl * w:(cl + 1) * w, 0:ng, :], semkey)

    def store_x(self, tg, dst):
        n = tg.n
        x = self.X(tg)
        if tg.sample:
            self.fm_to_dram_small(x, DC, NS,
                                  lambda cl, ng: dst.rearrange("t (g c f) -> t g c f", c=4, f=128)[:, :, cl, :],
                                  ("xout", 0))
            return
        nb = n // 128
        for b in range(nb):
            st = self.carve((b % 2) * 8192, [D], F32)
            for c4 in range(0, DC, 4):
                ps = self.psum()
                for j in range(4):
                    c = c4 + j
                    self.tr(ps[:, j * 128:(j + 1) * 128], x[:, c, b * 128:(b + 1) * 128], self.ident[:])
                eng = "act" if (c4 // 4) % 2 == 0 else "dve"
                self.cp(eng, st[:, c4 * 128:(c4 + 4) * 128], ps[:, :])
            self.dma("sp", dst[b * 128:(b + 1) * 128, :], st[:, :], ("xout", b % 2))

    def rmsnorm(self, tg, gvec):
        n = tg.n
        x, h = self.X(tg), self.H(tg)
        ps = self.psum()
        for c in range(DC):
            self.act(h[:, c, 0:n], x[:, c, 0:n], AF.Square)
        for c in range(DC):
            self.mm(ps[:, 0:n], self.onesD[:], h[:, c, 0:n], start=(c == 0), stop=(c == DC - 1))
        rs = self.carve(self.SCR_BYTES - 2048, [512], F32)[:, 0:n]
        self.rsqrt_eps(rs, ps[:, 0:n])
        for c in range(DC):
            self.stt("dve", h[:, c, 0:n], x[:, c, 0:n], gvec[:, c:c + 1], rs, ALU.mult, ALU.mult)

    def lay_even(self):
        L = {}
        o = 0

        def put(name, shape, dt):
            nonlocal o
            n = 1
            for d_ in shape:
                n *= d_
            o = (o + 3) // 4 * 4
            L[name] = self.carve(o, shape, dt)
            o += n * _DSZ[dt]

        put("STG", [4, 512], F32)
        put("KN", [2, 512], BF16)
        put("DIAG", [31, 128], BF16)
        put("YSQ", [2, 512], BF16)
        put("SIG", [512], F32)
        put("SQ", [2, 512], BF16)
        put("T1", [512], F32)
        put("T2", [512], F32)
        put("T3", [512], F32)
        put("Qp", [12, TT], BF16)
        o = (o + 3) // 4 * 4
        self.yp_off = o
        put("Yp", [8, TT], BF16)
        put("YpPad", [256], BF16)
        put("Qs", [12, NS], BF16)
        put("Ys", [8, NS], BF16)
        put("As", [8, SB, 38], BF16)
        put("A32s", [8, NS], F32)
        oA = o
        put("Ap", [8, TT + 32], BF16)
        o = oA
        put("ACCN", [512], F32)
        put("ACCD", [512], F32)
        put("PT", [2, 512], BF16)
        put("RDEN", [512], F32)
        assert o <= self.SCR_BYTES, o
        return L

    def alloc_kv(self):
        self.KT0 = self.sb("KT0", [128, 4, 5, 128], BF16)
        self.V0 = self.sb("V0", [128, 5, 512], BF16)
        self.KT1 = self.sb("KT1", [128, 2, 4, 4, 128], BF16)
        self.V1 = self.sb("V1", [128, 2, 4, 512], BF16)
        self.KT2 = self.sb("KT2", [128, 4, 4, 4, 128], BF16)
        self.V2 = self.sb("V2", [128, 4, 4, 512], BF16)
        self.atail = self.sb("atail", [128, 8, 30])
        self.ahist = self.sb("ahist", [128, 8, 30], BF16)
        self.ss4 = self.sb("ss4", [128, 4])
        self.KTs = self.sb("KTs", [128, 12, 128], BF16)
        self.Vs = self.sb("Vs", [128, 3, 512], BF16)
        self.stgctr = 0
        self.cbctr = 0

    def kt(self, g, i, qb, h):
        if g == 0:
            return self.KT0[:, h, 1 + qb, :]
        if g == 1:
            return self.KT1[:, i % 2, h, qb, :]
        return self.KT2[:, i, h, qb, :]

    def vv(self, g, i, qb):
        if g == 0:
            return self.V0[:, 1 + qb, :]
        if g == 1:
            return self.V1[:, i % 2, qb, :]
        return self.V2[:, i, qb, :]

    @staticmethod
    def bcols(ap, g, qb):
        if g == 0:
            return ap[:, qb * 128:(qb + 1) * 128]
        return ap[:, qb::4]

    def even_mixer(self, tgs, W):
        L = self.lay_even()
        w_in = W["w_in_e"]
        last = any((not t.sample) and t.tile == NT - 1 for t in tgs)

        def Aview(tg, c, j):
            if tg.sample:
                return L["As"][:, c, :, j:j + DS]
            return L["Ap"][:, c, j:j + tg.n]

        for tg in tgs:
            if tg.sample:
                self.load_state(L)
            elif tg.tile == 0:
                self.memset("pool", L["Ap"][:, :, 0:30], 0.0)
            else:
                self.cp("pool", L["Ap"][:, :, 0:30], self.ahist[:])
        for c in range(8):
            wv = self.wload(w_in[:, c * 128:(c + 1) * 128])
            wg = self.wload(w_in[:, 1024 + c * 128:1024 + (c + 1) * 128])
            for tg in tgs:
                n = tg.n
                h = self.H(tg)
                pv, pg = self.psum(), self.psum()
                for kc in range(DC):
                    self.mm(pv[:, 0:n], wv[:, kc, :], h[:, kc, 0:n], start=(kc == 0), stop=(kc == DC - 1))
                for kc in range(DC):
                    self.mm(pg[:, 0:n], wg[:, kc, :], h[:, kc, 0:n], start=(kc == 0), stop=(kc == DC - 1))
                sig = L["SIG"][:, 0:n]
                self.act(sig, pg[:, 0:n], AF.Sigmoid)
                if tg.sample:
                    a32 = L["A32s"][:, c, :]
                    self.tt("dve", a32, pv[:, 0:n], sig, ALU.mult)
                    self.cp("pool", L["As"][:, c, :, 30:38], a32.rearrange("p (b t) -> p b t", t=DS))
                else:
                    self.tt("dve", L["Ap"][:, c, 30:30 + n], pv[:, 0:n], sig, ALU.mult)
                    if tg.tile == NT - 1:
                        self.tt("dve", self.atail[:, c, :], pv[:, n - 30:n], sig[:, n - 30:n], ALU.mult)

        if self.stage <= 1:
            return
        for hq in range(12):
            g = hq // 4
            wq = self.wload(w_in[:, 2048 + hq * 128:2048 + (hq + 1) * 128])
            for tg in tgs:
                n = tg.n
                h = self.H(tg)
                Q = L["Qs"] if tg.sample else L["Qp"]
                pq = self.psum()
                for kc in range(DC):
                    self.mm(pq[:, 0:n], wq[:, kc, :], h[:, kc, 0:n], start=(kc == 0), stop=(kc == DC - 1))
                sq = L["SQ"][:, hq % 2, 0:n]
                self.act(sq, pq[:, 0:n], AF.Square)
                pm = self.psC
                self.mm(pm[:, 0:n], self.onesH[:], sq, start=True, stop=True)
                rs = L["T1"][:, 0:n]
                self.rsqrt_eps(rs, pm[:, 0:n])
                self.stt("dve", Q[:, hq, 0:n], pq[:, 0:n], self.qn_e[:, g:g + 1], rs, ALU.mult, ALU.mult)

        if self.stage <= 2:
            return
        import os
        dbg = os.environ.get("DBG", "")
        for g in range(3):
            if "g0only" in dbg and g > 0:
                continue
            for kv in range(2):
                col0 = 2048 + 1536 * (1 + kv) + g * 512
                banks = {}
                for hh in range(4):
                    ws = self.wload(w_in[:, col0 + hh * 128:col0 + (hh + 1) * 128])
                    for tg in tgs:
                        if "nosamplekv" in dbg and tg.sample:
                            continue
                        if "nopromptkv" in dbg and not tg.sample:
                            continue
                        h = self.H(tg)
                        nblk = 1 if tg.sample else 4
                        for qb in range(nblk):
                            if tg.sample:
                                bank = self.psA
                                rows = 128
                                lhs = lambda kc: h[:, kc, 0:128]
                            else:
                                bank = self.ps[qb]
                                rows = 128
                                lhs = (lambda kc, qb=qb, h=h: self.bcols(h[:, kc, :], g, qb))
                            for kc in range(DC):
                                self.mm(bank[0:rows, hh * 128:(hh + 1) * 128], lhs(kc), ws[:, kc, :],
                                        start=(hh == 0 and kc == 0), stop=(kc == DC - 1), skip=True)
                            banks[(tg.name, qb)] = (tg, qb, bank, rows)
                for (tg, qb, bank, rows) in banks.values():
                    if "noevac" in dbg:
                        continue
                    if ("nok" in dbg and kv == 0) or ("nov" in dbg and kv == 1):
                        continue
                    if kv == 0:
                        self.k_block(L, tg, g, qb, bank, rows)
                    else:
                        self.v_block(L, tg, g, qb, bank, rows)
        if self.stage <= 3:
            return
        self.even_conv(tgs, L, Aview)
        if self.stage <= 4:
            return
        for tg in tgs:
            if tg.sample:
                if self.stage >= 6:
                    self.even_attn_sample(L)
            else:
                self.even_attn_prompt(tg, L)
        if self.stage <= 6:
            return
        self.even_out(tgs, L, W)

    def kv_out_dma(self, tg, g, qb, kv, src, rows, slot):
        names = [["k128", "v128"], ["k512", "v512"], ["k2048", "v2048"]]
        key = ("stg", slot)
        import os
        dbg = os.environ.get("DBG", "")
        if ("skipS" in dbg and tg.sample) or ("skipP" in dbg and not tg.sample):
            return
        if tg.sample:
            dst = self.outs[names[g][kv] + "_s"].ap()
            self.dma("sp", dst, src[0:NS, :], key)
            return
        i = tg.tile
        dst = self.outs[names[g][kv] + "_p"].ap()
        if g == 0:
            if i == NT - 1 and qb == 3:
                self.dma("sp", dst, src, key)
        elif g == 1:
            if i == NT - 1:
                self.dma("sp", dst[qb:qb + 4 * 127 + 1:4, :], src, key)
        else:
            r0 = 512 * i + qb
            self.dma("sp", dst[r0:r0 + 4 * 127 + 1:4, :], src, key)

    def kv_needs_out(self, tg, g, qb):
        if tg.sample:
            return True
        if g == 0:
            return tg.tile == NT - 1 and qb == 3
        if g == 1:
            return tg.tile == NT - 1
        return True

    def k_block(self, L, tg, g, qb, bank, rows):
        psK = bank[0:rows, :]
        t2 = L["T2"][0:rows, :]
        self.act(t2, psK, AF.Square)
        ss = self.ss4[0:rows, :]
        self.s.add("dve", lambda e: e.tensor_reduce(ss, t2.rearrange("p (h d) -> p h d", d=128), AX.X, ALU.add),
                   reads=[t2], writes=[ss])
        self.act(ss, ss, AF.Sqrt, bias=self.epsb[0:rows, :], scale=1.0 / 128)
        self.recip(ss, ss)
        slot = self.stgctr % 4
        self.stgctr += 1
        stg = L["STG"][0:rows, slot, :]
        stg3 = stg.rearrange("p (h d) -> p h d", d=128)
        self.tt("dve", stg3, psK.rearrange("p (h d) -> p h d", d=128), ss.unsqueeze(2).broadcast_to([rows, 4, 128]), ALU.mult)
        self.tt("dve", stg3, stg3, self.kn_e[g][0:rows, :].unsqueeze(1).broadcast_to([rows, 4, 128]), ALU.mult)
        kn = L["KN"][0:rows, self.stgctr % 2, :]
        self.cp("act", kn, stg)
        import os
        dbg = os.environ.get("DBG", "")
        if self.kv_needs_out(tg, g, qb) and "nodma" not in dbg:
            self.kv_out_dma(tg, g, qb, 0, stg, rows, slot)
        if "notr" in dbg:
            return
        pt = self.psT
        for hh in range(4):
            self.tr(pt[:, hh * 128:(hh + 1) * 128], kn[:, hh * 128:(hh + 1) * 128], self.identb[:])
        src = pt[:, 0:512].rearrange("p (h t) -> p h t", t=128)
        if tg.sample:
            self.cp("dve", self.KTs[:, 4 * g:4 * g + 4, :], src)
        else:
            i = tg.tile
            if g == 0:
                dst = self.KT0[:, :, 1 + qb, :]
            elif g == 1:
                dst = self.KT1[:, i % 2, :, qb, :]
            else:
                dst = self.KT2[:, i, :, qb, :]
            self.cp("dve", dst, src)

    def v_block(self, L, tg, g, qb, bank, rows):
        psV = bank[0:rows, :]
        if tg.sample:
            self.cp("act", self.Vs[0:rows, g, :], psV)
        else:
            self.cp("act", self.vv(g, tg.tile, qb), psV)
        if self.kv_needs_out(tg, g, qb):
            slot = self.stgctr % 4
            self.stgctr += 1
            stg = L["STG"][0:rows, slot, :]
            self.cp("dve", stg, psV)
            self.kv_out_dma(tg, g, qb, 1, stg, rows, slot)

    def even_conv(self, tgs, L, Aview):
        for c in range(8):
            dg = L["DIAG"]
            wc = self.conv_w[:, c * 31:(c + 1) * 31]
            self.tt("dve", dg, self.identb[:].unsqueeze(1).broadcast_to([128, 31, 128]),
                    wc.unsqueeze(2).broadcast_to([128, 31, 128]), ALU.mult)
            for tg in tgs:
                n = tg.n
                py = self.psum()
                if tg.sample:
                    pyv = py[:, 0:n].rearrange("p (b t) -> p b t", t=DS)
                else:
                    pyv = py[:, 0:n]
                for j in range(CW):
                    self.mm(pyv, dg[:, j, :], Aview(tg, c, j), start=(j == 0), stop=(j == CW - 1))
                Y = L["Ys"] if tg.sample else L["Yp"]
                self.act(Y[:, c, 0:n], py[:, 0:n], AF.Identity, bias=self.conv_b[:, c:c + 1], scale=1.0)
                ysq = L["YSQ"][:, c % 2, 0:n]
                self.act(ysq, py[:, 0:n], AF.Square, bias=self.conv_b[:, c:c + 1], scale=1.0)
                pa = self.psA if not tg.sample else self.psC
                off = 0 if not tg.sample else 64
                if tg.sample:
                    self.mm(pa[:, 0:n], self.onesC[:], Y[:, c, 0:n], start=(c == 0), stop=(c == 7), skip=True)
                    self.mm(pa[:, 64:64 + n], self.onesC[:], ysq, start=False, stop=(c == 7), skip=True)
                else:
                    self.mm(self.psA[:, 0:n], self.onesC[:], Y[:, c, 0:n], start=(c == 0), stop=(c == 7))
                    self.mm(self.psB[:, 0:n], self.onesC[:], ysq, start=(c == 0), stop=(c == 7))
        for tg in tgs:
            n = tg.n
            if tg.sample:
                mean, ex2 = self.psC[:, 0:n], self.psC[:, 64:64 + n]
            else:
                mean, ex2 = self.psA[:, 0:n], self.psB[:, 0:n]
            Y = L["Ys"] if tg.sample else L["Yp"]
            MIX = self.H(tg)
            t1, t2, t3 = L["T1"][:, 0:n], L["T2"][:, 0:n], L["T3"][:, 0:n]
            self.act(t1, mean, AF.Square)
            self.tt("dve", t2, ex2, t1, ALU.subtract)
            self.rsqrt_eps(t2, t2)
            self.stt("dve", t3, mean, -1.0, t2, ALU.mult, ALU.mult)
            for c in range(8):
                yn = L["SIG"][:, 0:n]
                self.tt("dve", yn, Y[:, c, 0:n], t2, ALU.mult)
                self.tt("dve", yn, yn, t3, ALU.add)
                self.act(MIX[:, c, 0:n], yn, AF.Silu, bias=self.cln_b[:, c:c + 1], scale=self.cln_g[:, c:c + 1])
            if tg.sample:
                self.conv_s_out(L)
            elif tg.tile == NT - 1:
                self.conv_p_out(L)
            else:
                self.cp("pool", self.ahist[:], L["Ap"][:, :, n:n + 30])

    def load_state(self, L):
        st = self.carve(0, [1024], F32)
        self.dma("sp", st[0:120, :], self.dram["sconv"].ap().rearrange("b j f -> (b j) f"), ("st", 0))
        for c4 in range(0, 8, 4):
            ps = self.psum()
            for j in range(4):
                c = c4 + j
                self.tr(ps[:, j * 128:j * 128 + 120], st[0:120, c * 128:(c + 1) * 128], self.ident[0:120, 0:120])
            for j in range(4):
                self.cp("dve", L["As"][:, c4 + j, :, 0:30],
                        ps[:, j * 128:j * 128 + 120].rearrange("p (b t) -> p b t", t=30))

    def conv_p_out(self, L):
        dst = self.outs["conv_p"].ap()
        self.fm_to_dram_small(self.atail, 8, 30,
                              lambda cl, ng: dst.rearrange("t (g c f) -> t g c f", c=4, f=128)[:, :, cl, :], ("stg", 0))
        self.stgctr = 2

    def conv_s_out(self, L):
        dst = self.outs["conv_s"].ap()
        ps = self.psum()
        st = self.carve(0, [4, 128], F32)
        for g in range(2):
            self.tr(ps[:, g * 128:(g + 1) * 128], L["A32s"][:, 4 * g:4 * g + 4, :], self.ident[:])
        self.cp("dve", st[:, 0:2, :], ps[:, 0:256].rearrange("p (g f) -> p g f", f=128))
        d4 = dst.rearrange("b j (g c f) -> b j g c f", c=4, f=128)
        for cl in range(4):
            for b in range(SB):
                self.dma("sp", d4[b, 22:30, :, cl, :], st[cl * NS + b * DS:cl * NS + (b + 1) * DS, 0:2, :], ("stg", 0))
        self.dma("sp", dst[:, 0:22, :], self.dram["sconv"].ap()[:, 8:30, :], ("d2d", 0))
        self.stgctr = 2

    def even_attn_prompt(self, tg, L):
        i = tg.tile
        Q = L["Qp"]
        MIX = self.h
        scale = 128.0 ** -0.5
        sbank = [self.ps[0], self.ps[1]]
        sctr = 0
        for hh in range(4):
            accn, accd = L["ACCN"], L["ACCD"]
            for g in range(3):
                hq = 4 * g + hh
                pieces = []
                for qb in range(4):
                    if g == 0:
                        if i > 0 or qb > 0:
                            pieces.append((qb, self.KT0[:, hh, qb, :], self.V0[:, qb, hh * 128:(hh + 1) * 128], self.m_prev))
                        pieces.append((qb, self.KT0[:, hh, 1 + qb, :], self.V0[:, 1 + qb, hh * 128:(hh + 1) * 128], self.m_own))
                    elif g == 1:
                        if i > 0:
                            pieces.append((qb, self.KT1[:, (i - 1) % 2, hh, qb, :],
                                           self.V1[:, (i - 1) % 2, qb, hh * 128:(hh + 1) * 128], self.m_prev))
                        pieces.append((qb, self.KT1[:, i % 2, hh, qb, :], self.V1[:, i % 2, qb, hh * 128:(hh + 1) * 128], self.m_own))
                    else:
                        for j in range(i):
                            pieces.append((qb, self.KT2[:, j, hh, qb, :], self.V2[:, j, qb, hh * 128:(hh + 1) * 128], self.m_bdf))
                        pieces.append((qb, self.KT2[:, i, hh, qb, :], self.V2[:, i, qb, hh * 128:(hh + 1) * 128], self.m_bdc))
                pnum, pden = self.ps[2], self.ps[3]
                first = True
                for p0 in range(0, len(pieces), 4):
                    grp = pieces[p0:p0 + 4]
                    sb_ = sbank[sctr % 2]
                    pt = L["PT"][:, sctr % 2, :]
                    sctr += 1
                    for k, (qb, ktap, vap, mask) in enumerate(grp):
                        qcols = self.bcols(Q[:, hq, :], g, qb)
                        o_ = sb_[:, k * 128:(k + 1) * 128]
                        self.mm(o_, ktap, qcols, start=True, stop=True)
                    w = 128 * len(grp)
                    self.act(pt[:, 0:w], sb_[:, 0:w], AF.Exp, scale=scale)
                    for k, (qb, ktap, vap, mask) in enumerate(grp):
                        pk = pt[:, k * 128:(k + 1) * 128]
                        self.tt("pool", pk, pk, mask, ALU.mult)
                        on = pnum[:, qb * 128:(qb + 1) * 128]
                        od = pden[:, qb * 128:(qb + 1) * 128]
                        self.mm(on, vap, pk, start=first, stop=True, skip=True)
                        self.mm(od, self.ones1[:], pk, start=first, stop=True, skip=True)
                        first = False
                if g == 0:
                    self.cp("act", accn, pnum[:, :])
                    self.cp("dve", accd, pden[:, :])
                else:
                    vn = accn.rearrange("p (s r) -> p r s", r=4)
                    vd = accd.rearrange("p (s r) -> p r s", r=4)
                    pn = pnum[:, :].rearrange("p (r s) -> p r s", s=128)
                    pd = pden[:, :].rearrange("p (r s) -> p r s", s=128)
                    self.tt("dve", vn, pn, vn, ALU.add)
                    self.tt("dve", vd, pd, vd, ALU.add)
            rd = L["RDEN"]
            self.recip(rd, accd)
            self.tt("dve", MIX[:, 8 + hh, :], accn, rd, ALU.mult)
        if i < NT - 1:
            self.cp("pool", self.KT0[:, :, 0, :], self.KT0[:, :, 4, :])
            self.cp("pool", self.V0[:, 0, :], self.V0[:, 4, :])

    def cache_block(self, CB, kd, vd, mask, b, qsel, pnum, pden, first):
        k = self.cbctr % 2
        self.cbctr += 1
        kc32 = CB["KC32"][:, k, :]
        self.dma("sp", kc32, kd, ("kc", k))
        vc = CB["VC"][:, k, :]
        self.dma("pool", vc, vd, ("vc", k))
        ps = self.psum()
        for hh in range(4):
            self.tr(ps[:, hh * 128:(hh + 1) * 128], kc32[:, hh * 128:(hh + 1) * 128], self.ident[:])
        ktc = CB["KTC"][:, k, :]
        self.cp("act", ktc, ps[:, :])
        pS = self.psC
        for hh in range(4):
            self.mm(pS[:, hh * DS:(hh + 1) * DS], ktc[:, hh * 128:(hh + 1) * 128], qsel(hh), start=True, stop=True)
        pt = CB["PTS"][:, k, 0:4 * DS]
        self.act(pt, pS[:, 0:4 * DS], AF.Exp, scale=128.0 ** -0.5)
        if mask is not None:
            pt3 = pt.rearrange("p (h t) -> p h t", t=DS)
            self.tt("dve", pt3, pt3, mask.unsqueeze(1).broadcast_to([128, 4, DS]), ALU.mult)
        for hh in range(4):
            c0 = hh * NS + b * DS
            pk = pt[:, hh * DS:(hh + 1) * DS]
            self.mm(pnum[:, c0:c0 + DS], vc[:, hh * 128:(hh + 1) * 128], pk, start=first, stop=True, skip=True)
            self.mm(pden[:, c0:c0 + DS], self.ones1[:], pk, start=first, stop=True, skip=True)
            first = False
        return first

    def lay_cb(self, off):
        CB = {}
        CB["KC32"] = self.carve(off, [2, 512], F32)
        CB["KTC"] = self.carve(off + 4096, [2, 512], BF16)
        CB["VC"] = self.carve(off + 6144, [2, 512], BF16)
        CB["PTS"] = self.carve(off + 8192, [2, 128], BF16)
        CB["end"] = off + 8192 + 512
        return CB

    def even_attn_sample(self, L):
        Qs = L["Qs"]
        CB = self.lay_cb(self.yp_off)
        pnum, pden = self.psA, self.psB
        first = True
        caches = [("ck128", "cv128"), ("ck512", "cv512"), ("ck2048", "cv2048")]
        for g in range(3):
            kd_all = self.dram[caches[g][0]].ap()
            vd_all = self.dram[caches[g][1]].ap()
            for b in range(SB):
                if g == 0:
                    blocks = [(slice(0, 128), 0)]
                elif g == 1:
                    blocks = [(slice(128 * j, 128 * j + 128), 1 + j) for j in range(4)]
                else:
                    blocks = [(slice(r, 2048, 16), 5 + r) for r in range(8)]
                for sl, mi in blocks:
                    first = self.cache_block(CB, kd_all[b, sl, :], vd_all[b, sl, :], self.m_sc[mi], b,
                                             (lambda hh, g=g, b=b: Qs[:, 4 * g + hh, b * DS:(b + 1) * DS]),
                                             pnum, pden, first)
            pS = self.psC
            for hh in range(4):
                self.mm(pS[:, hh * NS:(hh + 1) * NS], self.KTs[:, 4 * g + hh, :], Qs[:, 4 * g + hh, :], start=True, stop=True)
            pt = CB["PTS"][:, g % 2, :]
            self.act(pt, pS[:, 0:4 * NS], AF.Exp, scale=128.0 ** -0.5)
            pt3 = pt.rearrange("p (h t) -> p h t", t=NS)
            self.tt("dve", pt3, pt3, self.m_sn[g].unsqueeze(1).broadcast_to([128, 4, NS]), ALU.mult)
            for hh in range(4):
                pk = pt[:, hh * NS:(hh + 1) * NS]
                vap = self.Vs[:, g, hh * 128:(hh + 1) * 128]
                self.mm(pnum[:, hh * NS:(hh + 1) * NS], vap, pk, start=False, stop=True, skip=True)
                self.mm(pden[:, hh * NS:(hh + 1) * NS], self.ones1[:], pk, start=False, stop=True, skip=True)
        rd = L["T1"][:, 0:4 * NS]
        self.recip(rd, pden[:, 0:4 * NS])
        self.tt("dve", self.hs[:, 8:12, 0:NS], pnum[:, 0:4 * NS].rearrange("p (h t) -> p h t", t=NS),
                rd.rearrange("p (h t) -> p h t", t=NS), ALU.mult)

    def even_out(self, tgs, L, W):
        w_out = W["w_out_e"]
        for oc in range(DC):
            ws = self.wload(w_out[:, oc * 128:(oc + 1) * 128])
            for tg in tgs:
                n = tg.n
                MIX, x = self.H(tg), self.X(tg)
                ps = self.psum()
                for kc in range(12):
                    self.mm(ps[:, 0:n], ws[:, kc, :], MIX[:, kc, 0:n], start=(kc == 0), stop=(kc == 11))
                self.tt("dve", x[:, oc, 0:n], ps[:, 0:n], x[:, oc, 0:n], ALU.add)


W_SHAPES = {
    "w_in_e": [2048, 6656], "w_out_e": [1536, 2048], "w_in_o": [2048, 4096], "w_out_o": [2048, 2048],
    "wq_mem": [2, 2048, 512], "wk_mem": [2, 2048, 512], "wv_mem": [2, 2048, 512], "wo_mem": [2, 512, 2048],
    "w_ffn1": [2, 2048, 8192], "w_ffn2": [2, 8192, 2048], "w_s": [8, 128, 128],
}


def build(nc, stage=99):
    B = Builder(nc, stage)
    B.setup()
    B.alloc_main()
    B.alloc_kv()
    xp = B.din("xp", [SEQ, D])
    xs = B.din("xs", [NS, D])
    B.din("mem", [NMEM, D])
    B.din("sconv", [SB, 30, DCONV])
    for w_, L_ in (("128", 128), ("512", 512), ("2048", 2048)):
        B.din("ck" + w_, [SB, L_, 512])
        B.din("cv" + w_, [SB, L_, 512])
    B.din("cmk", [2, SB, NMEM, 512])
    B.din("cmv", [2, SB, NMEM, 512])
    W = {}
    for k, shp in W_SHAPES.items():
        W[k] = B.din(k, shp).ap()
    yp = B.dout("y_p", [SEQ, D])
    ys = B.dout("y_s", [NS, D])
    B.dout("conv_p", [30, DCONV])
    B.dout("conv_s", [SB, 30, DCONV])
    for w_, L_ in (("128", 128), ("512", 512), ("2048", 2048)):
        B.dout("k%s_p" % w_, [L_, 512])
        B.dout("v%s_p" % w_, [L_, 512])
    for w_ in ("128", "512", "2048"):
        B.dout("k%s_s" % w_, [NS, 512])
        B.dout("v%s_s" % w_, [NS, 512])
    B.dout("memk_p", [2, NMEM, 512])
    B.dout("memv_p", [2, NMEM, 512])
    B.dout("chunkv_s", [NS, D])

    passes = [[B.tgP[0], B.tgS], [B.tgP[1]], [B.tgP[2]], [B.tgP[3]]]
    if stage < 10:
        passes = passes[:1]
    for pi, tgs in enumerate(passes):
        for tg in tgs:
            if tg.sample:
                B.load_x(tg, xs.ap())
            else:
                B.load_x(tg, xp.ap()[tg.tile * TT:(tg.tile + 1) * TT, :])
        for l in range(2):
            if l == 0:
                for tg in tgs:
                    B.rmsnorm(tg, B.g_mix[0])
                B.even_mixer(tgs, W)
            elif stage >= 13 or stage == 9:
                B.odd_mixer(tgs, W)
            if stage in (11, 12) or stage >= 14 or stage == 8:
                B.mem_attn(tgs, W, l, pi == 0)
            if stage == 12 or stage >= 14:
                B.ffn(tgs, W, l)
            if stage < 13 and stage != 9:
                break
        for tg in tgs:
            if tg.sample:
                B.store_x(tg, ys.ap())
            else:
                B.store_x(tg, yp.ap()[tg.tile * TT:(tg.tile + 1) * TT, :])
    B.s.emit()
    return B


def host_masks():
    m = np.zeros((128, 4 * 128 + 13 * 8 + 3 * 32 + 128), np.float32)
    k = np.arange(128)[:, None]
    q = np.arange(128)[None, :]
    m[:, 0:128] = (k <= q)
    m[:, 128:256] = (k > q)
    m[:, 256:384] = (k % 4 == q % 4)
    m[:, 384:512] = (k % 4 == q % 4) & (k // 4 <= q // 4)
    t = np.arange(8)[None, :]
    p = np.arange(128)[:, None]
    o = 512
    m[:, o:o + 8] = (p >= t + 1); o += 8
    for j in range(4):
        i_ = 128 * j + p
        m[:, o:o + 8] = (i_ >= t + 4) & (i_ % 4 == t % 4); o += 8
    for r in range(8):
        m[:, o:o + 8] = (t == r) & (p >= 1); o += 8
    kk = np.arange(32)[:, None]
    qq = np.arange(32)[None, :]
    same_b = (kk // 8 == qq // 8)
    u, tt_ = kk % 8, qq % 8
    m[0:32, o:o + 32] = same_b & (u <= tt_); o += 32
    m[0:32, o:o + 32] = same_b & (u <= tt_) & ((tt_ - u) % 4 == 0); o += 32
    m[0:32, o:o + 32] = same_b & (u == tt_); o += 32
    m[:, o:o + 128] = (q <= k); o += 128
    import ml_dtypes
    return m.astype(ml_dtypes.bfloat16)


def fm(vec):
    v = np.asarray(vec, np.float32)
    return np.ascontiguousarray(v.reshape(-1, 128).T)


def make_in_maps(inp):
    f32 = lambda a: np.ascontiguousarray(np.asarray(a, np.float32))
    cols = []
    for name in ("g_mix", "g_xmem", "g_ffn", "g_mem"):
        for l in range(2):
            cols.append(fm(inp[name][l]))
    cw = np.asarray(inp["conv_w"][0], np.float32)
    cols.append(np.ascontiguousarray(cw.T.reshape(8, 128, 31).transpose(1, 0, 2).reshape(128, 248)))
    cols.append(fm(inp["conv_b"][0]))
    cols.append(fm(inp["conv_ln_g"][0]))
    cols.append(fm(inp["conv_ln_b"][0]))
    cols.append(np.ascontiguousarray(np.asarray(inp["q_norm_e"][0], np.float32).T))
    cols.append(np.ascontiguousarray(np.asarray(inp["q_norm_mem"], np.float32).T))
    cols.append(fm(np.asarray(inp["b_in_o"][0])[:2048]))
    cols.append(fm(inp["v_ln_g"][0]))
    cols.append(fm(inp["v_ln_b"][0]))
    vecs = np.ascontiguousarray(np.concatenate(cols, axis=1))
    rows = np.concatenate([np.asarray(inp["k_norm_e"][0], np.float32).reshape(-1),
                           np.asarray(inp["k_norm_mem"], np.float32).reshape(-1),
                           np.asarray(inp["b_s"][0], np.float32).reshape(-1)])
    rows = np.ascontiguousarray(np.broadcast_to(rows[None, :], (128, rows.size)))
    shared = {
        "vecs": vecs, "rows": rows, "c_masks": host_masks(), "c_ident": np.eye(128, dtype=np.float32),
        "w_in_e": f32(inp["w_in_e"][0]), "w_out_e": f32(inp["w_out_e"][0]),
        "w_in_o": f32(inp["w_in_o"][0]), "w_out_o": f32(inp["w_out_o"][0]),
        "wq_mem": f32(inp["wq_mem"]), "wk_mem": f32(inp["wk_mem"]), "wv_mem": f32(inp["wv_mem"]),
        "wo_mem": f32(inp["wo_mem"]), "w_ffn1": f32(inp["w_ffn1"]), "w_ffn2": f32(inp["w_ffn2"]),
        "w_s": f32(inp["w_s"][0]), "bvrow": f32(np.asarray(inp["b_in_o"][0])[2048:].reshape(1, 2048)),
    }
    maps = []
    for c in range(8):
        sl = slice(SB * c, SB * c + SB)
        m = dict(shared)
        m["xp"] = f32(inp["x_prompt"][c])
        m["xs"] = f32(np.asarray(inp["x_sample"][sl]).reshape(NS, D))
        m["mem"] = f32(inp["mem_prompt"][c])
        m["sconv"] = f32(inp["state_conv"][0, sl])
        for w_ in ("128", "512", "2048"):
            m["ck" + w_] = f32(np.asarray(inp["cache_k_w" + w_][0, sl]).reshape(SB, -1, 512))
            m["cv" + w_] = f32(np.asarray(inp["cache_v_w" + w_][0, sl]).reshape(SB, -1, 512))
        m["cmk"] = f32(np.asarray(inp["cache_mem_k"][:, sl]).reshape(2, SB, NMEM, 512))
        m["cmv"] = f32(np.asarray(inp["cache_mem_v"][:, sl]).reshape(2, SB, NMEM, 512))
        maps.append(m)
    return maps


def _lay(self, items):
    L = {}
    o = 0
    for name, shape, dt in items:
        n = 1
        for d_ in shape:
            n *= d_
        o = (o + 3) // 4 * 4
        L[name] = self.carve(o, shape, dt)
        o += n * _DSZ[dt]
    assert o <= self.SCR_BYTES - 2048, o
    return L


def mem_attn(self, tgs, W, l, first_pass):
    L = _lay(self, [
        ("MX", [DC, NMEM], F32), ("MH", [DC, NMEM], BF16), ("MKT", [4, NMEM], BF16), ("MV", [2, 512], BF16),
        ("STG", [2, 512], F32), ("KN", [2, 512], BF16), ("T2", [512], F32), ("T1", [512], F32),
        ("SQ", [2, 512], BF16), ("Qp", [4, TT], BF16), ("Qs", [4, NS], BF16), ("PT", [2, 512], BF16),
        ("Op", [4, TT], BF16), ("Os", [4, 128], BF16), ("RD", [512], F32),
        ("KC32", [2, 512], F32), ("KTC", [2, 512], BF16), ("VC", [2, 512], BF16), ("PTS", [2, 128], BF16),
    ])
    CB = {k: L[k] for k in ("KC32", "KTC", "VC", "PTS")}
    has_prompt = any(not t.sample for t in tgs)
    wq, wk, wv, wo = W["wq_mem"][l], W["wk_mem"][l], W["wv_mem"][l], W["wo_mem"][l]
    if has_prompt:
        memd = self.dram["mem"].ap()
        mx, mh = L["MX"], L["MH"]
        for b in range(2):
            st = self.carve(self.SCR_BYTES - 2048 - 8192 * (1 + b), [D], F32)
            self.dma("sp", st[:, :], memd[b * 128:(b + 1) * 128, :], ("xin", b))
            for c4 in range(0, DC, 4):
                ps = self.psum()
                for j in range(4):
                    self.tr(ps[:, j * 128:(j + 1) * 128], st[:, (c4 + j) * 128:(c4 + j + 1) * 128], self.ident[:])
                self.cp("act" if (c4 // 4) % 2 == 0 else "dve", mx[:, c4:c4 + 4, b * 128:(b + 1) * 128],
                        ps[:, 0:512].rearrange("p (j t) -> p j t", t=128))
        ps = self.psum()
        for c in range(DC):
            self.act(mh[:, c, :], mx[:, c, :], AF.Square)
        for c in range(DC):
            self.mm(ps[:, 0:NMEM], self.onesD[:], mh[:, c, :], start=(c == 0), stop=(c == DC - 1))
        rs = L["T1"][:, 0:NMEM]
        self.rsqrt_eps(rs, ps[:, 0:NMEM])
        for c in range(DC):
            self.stt("dve", mh[:, c, :], mx[:, c, :], self.g_mem[l][:, c:c + 1], rs, ALU.mult, ALU.mult)
        for kv in range(2):
            wsrc = wk if kv == 0 else wv
            banks = [self.ps[0], self.ps[1]]
            for hh in range(4):
                ws = self.wload(wsrc[:, hh * 128:(hh + 1) * 128])
                for b in range(2):
                    for kc in range(DC):
                        self.mm(banks[b][:, hh * 128:(hh + 1) * 128], mh[:, kc, b * 128:(b + 1) * 128], ws[:, kc, :],
                                start=(hh == 0 and kc == 0), stop=(kc == DC - 1), skip=True)
            for b in range(2):
                psb = banks[b]
                stg = L["STG"][:, b, :]
                dst = self.outs["memk_p" if kv == 0 else "memv_p"].ap()[l, b * 128:(b + 1) * 128, :]
                if kv == 0:
                    t2 = L["T2"]
                    self.act(t2, psb[:, :], AF.Square)
                    ss = self.ss4[:, :]
                    self.s.add("dve", (lambda e, ss=ss, t2=t2: e.tensor_reduce(ss, t2.rearrange("p (h d) -> p h d", d=128), AX.X, ALU.add)),
                               reads=[t2], writes=[ss])
                    self.act(ss, ss, AF.Sqrt, bias=self.epsb[:, :], scale=1.0 / 128)
                    self.recip(ss, ss)
                    stg3 = stg.rearrange("p (h d) -> p h d", d=128)
                    self.tt("dve", stg3, psb[:, :].rearrange("p (h d) -> p h d", d=128),
                            ss.unsqueeze(2).broadcast_to([128, 4, 128]), ALU.mult)
                    self.tt("dve", stg3, stg3, self.kn_m[l].unsqueeze(1).broadcast_to([128, 4, 128]), ALU.mult)
                    kn = L["KN"][:, b, :]
                    self.cp("act", kn, stg)
                    pt = self.psT
                    for hh in range(4):
                        self.tr(pt[:, hh * 128:(hh + 1) * 128], kn[:, hh * 128:(hh + 1) * 128], self.identb[:])
                    self.cp("dve", L["MKT"][:, :, b * 128:(b + 1) * 128], pt[:, 0:512].rearrange("p (h t) -> p h t", t=128))
                else:
                    self.cp("act", L["MV"][:, b, :], psb[:, :])
                    if first_pass:
                        self.cp("dve", stg, psb[:, :])
                if first_pass:
                    self.dma("sp", dst, stg, ("stg", b))
    for tg in tgs:
        self.rmsnorm(tg, self.g_xmem[l])
    for hh in range(4):
        ws = self.wload(wq[:, hh * 128:(hh + 1) * 128])
        for tg in tgs:
            n = tg.n
            h = self.H(tg)
            Q = L["Qs"] if tg.sample else L["Qp"]
            pq = self.psum()
            for kc in range(DC):
                self.mm(pq[:, 0:n], ws[:, kc, :], h[:, kc, 0:n], start=(kc == 0), stop=(kc == DC - 1))
            sq = L["SQ"][:, hh % 2, 0:n]
            self.act(sq, pq[:, 0:n], AF.Square)
            pm = self.psC
            self.mm(pm[:, 0:n], self.onesH[:], sq, start=True, stop=True)
            rs = L["T1"][:, 0:n]
            self.rsqrt_eps(rs, pm[:, 0:n])
            self.stt("dve", Q[:, hh, 0:n], pq[:, 0:n], self.qn_m[:, l:l + 1], rs, ALU.mult, ALU.mult)
    for tg in tgs:
        n = tg.n
        if tg.sample:
            pnum, pden = self.psA, self.psB
            first = True
            kd_all, vd_all = self.dram["cmk"].ap(), self.dram["cmv"].ap()
            for b in range(SB):
                for blk in range(2):
                    first = self.cache_block(CB, kd_all[l, b, blk * 128:(blk + 1) * 128, :],
                                             vd_all[l, b, blk * 128:(blk + 1) * 128, :], None, b,
                                             (lambda hh, b=b: L["Qs"][:, hh, b * DS:(b + 1) * DS]), pnum, pden, first)
            rd = L["RD"][:, 0:4 * NS]
            self.recip(rd, pden[:, 0:4 * NS])
            self.tt("dve", L["Os"][:, :, 0:NS], pnum[:, 0:4 * NS].rearrange("p (h t) -> p h t", t=NS),
                    rd.rearrange("p (h t) -> p h t", t=NS), ALU.mult)
        else:
            sctr = 0
            for hh in range(4):
                pnum, pden = self.ps[2], self.ps[3]
                for blk in range(2):
                    sb_ = self.ps[sctr % 2]
                    pt = L["PT"][:, sctr % 2, :]
                    sctr += 1
                    self.mm(sb_[:, 0:n], L["MKT"][:, hh, blk * 128:(blk + 1) * 128], L["Qp"][:, hh, 0:n], start=True, stop=True)
                    self.act(pt[:, 0:n], sb_[:, 0:n], AF.Exp, scale=128.0 ** -0.5)
                    self.mm(pnum[:, 0:n], L["MV"][:, blk, hh * 128:(hh + 1) * 128], pt[:, 0:n], start=(blk == 0), stop=(blk == 1))
                    self.mm(pden[:, 0:n], self.ones1[:], pt[:, 0:n], start=(blk == 0), stop=(blk == 1))
                rd = L["RD"][:, 0:n]
                self.recip(rd, pden[:, 0:n])
                self.tt("dve", L["Op"][:, hh, 0:n], pnum[:, 0:n], rd, ALU.mult)
    for oc in range(DC):
        ws = self.wload(wo[:, oc * 128:(oc + 1) * 128])
        for tg in tgs:
            n = tg.n
            O = L["Os"] if tg.sample else L["Op"]
            x = self.X(tg)
            ps = self.psum()
            for kc in range(4):
                self.mm(ps[:, 0:n], ws[:, kc, :], O[:, kc, 0:n], start=(kc == 0), stop=(kc == 3))
            self.tt("dve", x[:, oc, 0:n], ps[:, 0:n], x[:, oc, 0:n], ALU.add)


def ffn(self, tgs, W, l):
    L = _lay(self, [("HIDp", [2, 16, TT], BF16), ("HIDs", [2, 16, NS], BF16), ("SQ", [2, 512], F32),
                    ("WS0", [16, 256], BF16), ("WS1", [16, 256], BF16), ("WS2", [16, 256], BF16)])
    slots = [L["WS0"], L["WS1"], L["WS2"]]
    w1, w2 = W["w_ffn1"][l], W["w_ffn2"][l]

    def wl(wap):
        i = self.wfctr % 3
        self.wfctr += 1
        dst = slots[i]
        self.dma("pool", dst, wap.rearrange("(kc p) n -> p kc n", p=128), ("wf", i))
        return dst

    for tg in tgs:
        self.rmsnorm(tg, self.g_ffn[l])
    ctr = 0
    for hb in range(4):
        for jj in range(8):
            ws = wl(w1[:, (hb * 16 + 2 * jj) * 128:(hb * 16 + 2 * jj + 2) * 128])
            for c2 in range(2):
                j = 2 * jj + c2
                for tg in tgs:
                    n = tg.n
                    h = self.H(tg)
                    HID = L["HIDs"] if tg.sample else L["HIDp"]
                    ps = self.psum()
                    for kc in range(DC):
                        self.mm(ps[:, 0:n], ws[:, kc, c2 * 128:(c2 + 1) * 128], h[:, kc, 0:n], start=(kc == 0), stop=(kc == DC - 1))
                    sq = L["SQ"][:, ctr % 2, 0:n]
                    ctr += 1
                    self.act(sq, ps[:, 0:n], AF.Square)
                    self.stt("dve", HID[:, hb % 2, j, 0:n], ps[:, 0:n], 0.0, sq, ALU.is_gt, ALU.mult)
        for oo in range(8):
            ws = wl(w2[hb * 2048:(hb + 1) * 2048, oo * 256:(oo + 1) * 256])
            for c2 in range(2):
                oc = 2 * oo + c2
                for tg in tgs:
                    n = tg.n
                    HID = L["HIDs"] if tg.sample else L["HIDp"]
                    x = self.X(tg)
                    ps = self.psum()
                    for kc in range(16):
                        self.mm(ps[:, 0:n], ws[:, kc, c2 * 128:(c2 + 1) * 128], HID[:, hb % 2, kc, 0:n], start=(kc == 0), stop=(kc == 15))
                    self.tt("dve", x[:, oc, 0:n], ps[:, 0:n], x[:, oc, 0:n], ALU.add)


Builder.mem_attn = mem_attn
Builder.ffn = ffn


def odd_mixer(self, tgs, W):
    L = _lay(self, [
        ("U", [16, TT], BF16), ("Us", [16, NS], BF16), ("VH", [4, 2048], BF16), ("VHs", [2048], BF16),
        ("Gs", [2048], F32), ("WST", [8, 128], BF16), ("BV", [2048], BF16),
        ("Z", [8, NS], BF16), ("CSB", [8, 128], BF16), ("CSBs", [8, NS], BF16), ("TMP", [512], F32),
        ("JK", [2, 512], BF16), ("ADDJ", [2, 128], F32), ("ST", [16], F32),
    ])
    L["WNB"] = L["TMP"].bitcast(BF16) if False else self.carve(int(L["TMP"].offset) % (self.SCR_BYTES // 4) * 4, [8, 128], BF16)
    L["CV"] = self.carve(int(L["JK"].offset) % (self.SCR_BYTES // 2) * 2, [16, NS], F32)
    w_in, w_out, w_s = W["w_in_o"], W["w_out_o"], W["w_s"]
    has_prompt = any(not t.sample for t in tgs)
    has_sample = any(t.sample for t in tgs)
    wn = L["Gs"][:, 0:1024].rearrange("p (g s) -> p g s", s=128)
    self.dma("sp", wn, w_s.rearrange("g t s -> t g s"), ("st", 0))
    self.tt("dve", L["WNB"], wn, self.m_low.unsqueeze(1).broadcast_to([128, 8, 128]), ALU.mult)
    pt = self.psT
    for g in range(8):
        self.tr(pt[:, g * 128:(g + 1) * 128], L["WNB"][:, g, :], self.identb[:])
    self.cp("act", L["WST"], pt[:, :].rearrange("p (g t) -> p g t", t=128))
    ps = self.psum()
    ps2 = self.psum()
    for g in range(8):
        bank = ps if g < 4 else ps2
        self.mm(bank[:, (g % 4) * 128:(g % 4 + 1) * 128], self.ones1[:], L["WST"][:, g, :], start=True, stop=True)
    self.cp("dve", L["CSB"][:, 0:4, :], ps[:, :].rearrange("p (g t) -> p g t", t=128))
    self.cp("dve", L["CSB"][:, 4:8, :], ps2[:, :].rearrange("p (g t) -> p g t", t=128))
    self.memset("pool", L["BV"], 0.0)
    self.s.add("pool", lambda e: e.dma_start(out=L["BV"][0:1, :], in_=self.dram["bvrow"].ap()), reads=[],
               writes=[L["BV"][0:1, :]], dma=True, semkey=("bv", 0))
    if has_sample:
        self.memset("pool", L["Z"], 0.0)
        for b in range(SB):
            for g in range(8):
                self.s.add("pool", (lambda e, b=b, g=g: e.dma_start(out=L["Z"][DS * b:DS * b + DS, g, DS * b:DS * b + DS],
                                                                   in_=w_s[g, 0:DS, 0:DS].rearrange("t s -> s t"),
                                                                   allow_slow_non_contiguous=True)),
                           reads=[], writes=[L["Z"][DS * b:DS * b + DS, g, DS * b:DS * b + DS]], dma=True, semkey=("z", 0))
        self.tt("dve", L["Z"], L["Z"], self.m_sn[0].unsqueeze(1).broadcast_to([128, 8, NS]), ALU.mult)
        ps = self.psum()
        for g in range(8):
            self.mm(ps[:, g * NS:(g + 1) * NS], self.ones1[:], L["Z"][:, g, :], start=True, stop=True)
        self.cp("dve", L["CSBs"], ps[:, 0:8 * NS].rearrange("p (g t) -> p g t", t=NS))
    for tg in tgs:
        self.rmsnorm(tg, self.g_mix[1])
    for j in range(16):
        ws = self.wload(w_in[:, j * 128:(j + 1) * 128])
        for tg in tgs:
            n = tg.n
            h = self.H(tg)
            U = L["Us"] if tg.sample else L["U"]
            ps = self.psum()
            for kc in range(DC):
                self.mm(ps[:, 0:n], ws[:, kc, :], h[:, kc, 0:n], start=(kc == 0), stop=(kc == DC - 1))
            self.act(U[:, j, 0:n], ps[:, 0:n], AF.Gelu, bias=self.b_u[:, j:j + 1], scale=1.0)
    blocks = []
    for tg in tgs:
        if tg.sample:
            blocks.append((tg, 0, self.psA))
        else:
            for qb in range(4):
                blocks.append((tg, qb, self.ps[qb]))
    for cg in range(4):
        for cc in range(4):
            col = cg * 512 + cc * 128
            ws = self.wload(w_in[:, 2048 + col:2048 + col + 128])
            for (tg, qb, bank) in blocks:
                h = self.H(tg)
                o_ = bank[:, cc * 128:(cc + 1) * 128]
                self.mm(o_, self.E0[:], L["BV"][:, col:col + 128], start=(cc == 0), stop=False, skip=True)
                for kc in range(DC):
                    self.mm(o_, h[:, kc, qb * 128:(qb + 1) * 128], ws[:, kc, :], start=False, stop=(kc == DC - 1), skip=True)
        for (tg, qb, bank) in blocks:
            if tg.sample:
                self.act(L["Gs"][:, cg * 512:(cg + 1) * 512], bank[:, :], AF.Gelu)
            else:
                self.act(L["VH"][:, qb, cg * 512:(cg + 1) * 512], bank[:, :], AF.Gelu)
    st = L["ST"]
    jk = 0
    for (tg, qb, bank) in blocks:
        src = L["Gs"] if tg.sample else L["VH"][:, qb, :]
        self.s.add("dve", (lambda e, src=src: e.tensor_reduce(st[:, 0:1], src, AX.X, ALU.add)), reads=[src], writes=[st[:, 0:1]])
        for cg in range(4):
            j_ = L["JK"][:, jk % 2, :]
            jk += 1
            self.act(j_, src[:, cg * 512:(cg + 1) * 512], AF.Square)
            self.s.add("dve", (lambda e, j_=j_, cg=cg: e.tensor_reduce(st[:, 4 + cg:5 + cg], j_, AX.X, ALU.add)),
                       reads=[j_], writes=[st[:, 4 + cg:5 + cg]])
        self.s.add("dve", lambda e: e.tensor_reduce(st[:, 1:2], st[:, 4:8], AX.X, ALU.add), reads=[st[:, 4:8]], writes=[st[:, 1:2]])
        self.ts("dve", st[:, 2:3], st[:, 0:1], 1.0 / 2048, None, ALU.mult)
        self.stt("dve", st[:, 3:4], st[:, 2:3], -1.0, st[:, 2:3], ALU.mult, ALU.mult)
        self.stt("dve", st[:, 3:4], st[:, 1:2], 1.0 / 2048, st[:, 3:4], ALU.mult, ALU.add)
        self.rsqrt_eps(st[:, 3:4], st[:, 3:4])
        if tg.sample:
            self.ts("dve", src, src, st[:, 2:3], st[:, 3:4], ALU.subtract, ALU.mult)
            self.cp("act", L["VHs"], src)
        else:
            self.ts("dve", src, src, st[:, 2:3], st[:, 3:4], ALU.subtract, ALU.mult)
    if has_sample:
        for c4 in range(0, 16, 4):
            ps = self.psum()
            for j in range(4):
                self.tr(ps[:, j * 128:(j + 1) * 128], L["Gs"][:, (c4 + j) * 128:(c4 + j + 1) * 128], self.ident[:])
            for j in range(4):
                c = c4 + j
                self.act(L["CV"][:, c, :], ps[:, j * 128:j * 128 + NS], AF.Identity,
                         bias=self.vln_b[:, c:c + 1], scale=self.vln_g[:, c:c + 1])
        dst = self.outs["chunkv_s"].ap()
        self.fm_to_dram_small(L["CV"], 16, NS, lambda cl, ng: dst.rearrange("t (g c f) -> t g c f", c=4, f=128)[:, :, cl, :],
                              ("stg", 0), st=L["TMP"].rearrange("p (g f) -> p g f", f=128))
    for j in range(16):
        g = j // 2
        for tg in tgs:
            n = tg.n
            U = L["Us"] if tg.sample else L["U"]
            ps = self.psum()
            addj = L["ADDJ"][:, j % 2, :]
            if tg.sample:
                self.mm(ps[:, 0:NS], L["VHs"][:, j * 128:(j + 1) * 128], L["Z"][:, g, :], start=True, stop=True)
                a_ = addj[:, 0:NS]
                self.stt("dve", a_.rearrange("p (b t) -> p b t", t=DS), L["CSBs"][:, g, :].rearrange("p (b t) -> p b t", t=DS),
                         self.vln_b[:, j:j + 1],
                         self.bs_row[:, g * 128:g * 128 + DS].unsqueeze(1).broadcast_to([128, SB, DS]), ALU.mult, ALU.add)
                tmp = L["TMP"][:, 0:NS]
                self.stt("dve", tmp, ps[:, 0:NS], self.vln_g[:, j:j + 1], a_, ALU.mult, ALU.add)
                self.tt("dve", U[:, j, 0:NS], tmp, U[:, j, 0:NS], ALU.mult)
            else:
                for qb in range(4):
                    self.mm(ps[:, qb * 128:(qb + 1) * 128], L["VH"][:, qb, j * 128:(j + 1) * 128], L["WST"][:, g, :], start=True, stop=True)
                self.stt("dve", addj, L["CSB"][:, g, :], self.vln_b[:, j:j + 1], self.bs_row[:, g * 128:(g + 1) * 128], ALU.mult, ALU.add)
                tmp = L["TMP"]
                self.stt("dve", tmp.rearrange("p (q t) -> p q t", t=128), ps[:, :].rearrange("p (q t) -> p q t", t=128),
                         self.vln_g[:, j:j + 1], addj.unsqueeze(1).broadcast_to([128, 4, 128]), ALU.mult, ALU.add)
                self.tt("dve", U[:, j, :], tmp, U[:, j, :], ALU.mult)
    for oc in range(DC):
        ws = self.wload(w_out[:, oc * 128:(oc + 1) * 128])
        for tg in tgs:
            n = tg.n
            U = L["Us"] if tg.sample else L["U"]
            x = self.X(tg)
            ps = self.psum()
            for kc in range(16):
                self.mm(ps[:, 0:n], ws[:, kc, :], U[:, kc, 0:n], start=(kc == 0), stop=(kc == 15))
            self.tt("dve", x[:, oc, 0:n], ps[:, 0:n], x[:, oc, 0:n], ALU.add)


Builder.odd_mixer = odd_mixer


_CACHE = {}


def kernel(**inputs):
    if "nc" not in _CACHE:
        nc = bass.Bass("TRN2", target_bir_lowering=False)
        build(nc, 99)
        _CACHE["nc"] = nc
    nc = _CACHE["nc"]
    maps = make_in_maps(inputs)
    res = run_bass_kernel_spmd(nc, maps, core_ids=list(range(8)))
    R = res.results
    g = lambda name: [np.asarray(r[name], dtype=np.float32) for r in R]
    y_p = np.stack(g("y_p"))
    y_s = np.concatenate([a.reshape(SB, DS, D) for a in g("y_s")], axis=0)
    conv_p = np.stack(g("conv_p"))[None]
    conv_s = np.concatenate(g("conv_s"), axis=0)[None]
    outs = [y_p, y_s, conv_p, conv_s]
    for w_, L_ in (("128", 128), ("512", 512), ("2048", 2048)):
        for kv in ("k", "v"):
            outs.append(np.stack([a.reshape(L_, 4, 128) for a in g("%s%s_p" % (kv, w_))])[None])
    for w_ in ("128", "512", "2048"):
        for kv in ("k", "v"):
            outs.append(np.concatenate([a.reshape(SB, DS, 4, 128) for a in g("%s%s_s" % (kv, w_))], axis=0)[None])
    outs.append(np.stack([a.reshape(2, NMEM, 4, 128) for a in g("memk_p")], axis=1))
    outs.append(np.stack([a.reshape(2, NMEM, 4, 128) for a in g("memv_p")], axis=1))
    outs.append(np.concatenate([a.reshape(SB, DS, D) for a in g("chunkv_s")], axis=0)[None])
    return tuple(outs)
```
